# Optimizing a Trainium2 kernel written in Bass

```python
import math
import jax, jax.numpy as jnp
from jax import lax
import numpy as np

D_MODEL = 4096
BATCH = 2
SEQ = 4096
DEPTH = 2

N_MIXERS = 2
N_LAYERS_A = (DEPTH + 1) // 2
N_LAYERS_B = DEPTH // 2

A_DK = 128
A_HEADS = D_MODEL // A_DK
A_DV = D_MODEL // A_HEADS
A_KEY_WIDTH = A_HEADS * A_DK
A_VAL_WIDTH = A_HEADS * A_DV
A_IN_WIDTH = 2 * A_KEY_WIDTH + 2 * A_VAL_WIDTH

B_DK = 128
B_DV = 128
B_HEADS = D_MODEL // B_DV
B_KEY_WIDTH = B_HEADS * B_DK
B_VAL_WIDTH = B_HEADS * B_DV
B_CONV_WIDTH = 4
B_CONV_CH = 2 * B_KEY_WIDTH + B_VAL_WIDTH
B_IN_WIDTH = B_CONV_CH + B_VAL_WIDTH + 2 * B_HEADS

CHUNK = 64
DEEPNORM_ALPHA = (2.0 * DEPTH) ** 0.25
DEEPNORM_BETA = (8.0 * DEPTH) ** -0.25
LN_EPS = 1e-5
RMS_EPS = 1e-6
L2_EPS = 1e-6

kernel_name = "hybrid_hgrn2_gated_deltanet_deepnorm"


def layer_norm(x, g, b):
    xf = x.astype(jnp.float32)
    mu = jnp.mean(xf, axis=-1, keepdims=True)
    var = jnp.mean(jnp.square(xf - mu), axis=-1, keepdims=True)
    return ((xf - mu) * lax.rsqrt(var + LN_EPS) * g.astype(jnp.float32) + b.astype(jnp.float32)).astype(x.dtype)


def gated_rmsnorm(o, z, w):
    b, s, h, dv = o.shape
    on = o * lax.rsqrt(jnp.mean(o * o, axis=-1, keepdims=True) + RMS_EPS) * w.astype(jnp.float32)
    return on.reshape(b, s, h * dv) * jax.nn.silu(z.astype(jnp.float32))


def to_chunks(x):
    b, s, h, d = x.shape
    return x.reshape(b, s // CHUNK, CHUNK, h, d).transpose(1, 0, 3, 2, 4)


def from_chunks(y):
    n, b, h, c, d = y.shape
    return y.transpose(1, 0, 3, 2, 4).reshape(b, n * c, h, d)


def causal_depthwise_conv(x, w):
    k = w.shape[0]
    return lax.conv_general_dilated(x, w[:, None, :], window_strides=(1,), padding=[(k - 1, 0)],
                                    dimension_numbers=("NWC", "WIO", "NWC"),
                                    feature_group_count=x.shape[-1])


def hgrn2_chunk_scan(q, k, v, log_f):
    n, b, h, c, dk = q.shape
    dv = v.shape[-1]
    causal = jnp.tril(jnp.ones((c, c), dtype=bool))

    def step(state, xs):
        q_c, k_c, v_c, g_c = xs
        cum = jnp.cumsum(g_c, axis=-2)
        diff = cum[..., :, None, :] - cum[..., None, :, :]
        decay = jnp.exp(jnp.where(causal[:, :, None], diff, -jnp.inf))
        attn = jnp.einsum("bhtd,bhsd,bhtsd->bhts", q_c, k_c, decay)
        o = (jnp.einsum("bhts,bhsv->bhtv", attn, v_c)
             + jnp.einsum("bhtd,bhdv->bhtv", q_c * jnp.exp(cum), state))
        last = cum[..., -1:, :]
        state = (jnp.exp(last[..., 0, :])[..., None] * state
                 + jnp.einsum("bhsd,bhsv->bhdv", k_c * jnp.exp(last - cum), v_c))
        return state, o

    state0 = jnp.zeros((b, h, dk, dv), jnp.float32)
    _, o = lax.scan(step, state0, (q, k, v, log_f))
    return o


def hgrn2_mixer(x, w_in, lb, norm_w, w_out):
    b, s, _ = x.shape
    hproj = x @ w_in
    q, f, i, z = jnp.split(hproj, [A_KEY_WIDTH, 2 * A_KEY_WIDTH, 2 * A_KEY_WIDTH + A_VAL_WIDTH], axis=-1)
    q = jax.nn.silu(q.astype(jnp.float32))
    forget = lb + (1.0 - lb) * jax.nn.sigmoid(f.astype(jnp.float32))
    k = 1.0 - forget
    log_f = jnp.log(forget)
    heads_k = lambda t: to_chunks(t.reshape(b, s, A_HEADS, A_DK))
    o = hgrn2_chunk_scan(heads_k(q), heads_k(k),
                         to_chunks(i.astype(jnp.float32).reshape(b, s, A_HEADS, A_DV)),
                         heads_k(log_f))
    y = gated_rmsnorm(from_chunks(o), z, norm_w)
    return y.astype(x.dtype) @ w_out


def gated_delta_chunk(q, k, v, beta, g):
    c = q.shape[-2]
    dv = v.shape[-1]
    strict = jnp.tril(jnp.ones((c, c), dtype=bool), -1)
    incl = jnp.tril(jnp.ones((c, c), dtype=bool))
    cum = jnp.cumsum(g, axis=-1)
    diff = cum[..., :, None] - cum[..., None, :]
    kk = jnp.einsum("nbhtd,nbhsd->nbhts", k, k)
    lower = beta[..., :, None] * kk * jnp.exp(jnp.where(strict, diff, -jnp.inf))
    eye = jnp.eye(c, dtype=jnp.float32)
    rhs = jnp.concatenate([beta[..., None] * v, (beta * jnp.exp(cum))[..., None] * k], axis=-1)
    sol = lax.linalg.triangular_solve(eye + lower, rhs, left_side=True, lower=True, unit_diagonal=True)
    u0, w = sol[..., :dv], sol[..., dv:]
    qk = jnp.einsum("nbhtd,nbhsd->nbhts", q, k) * jnp.exp(jnp.where(incl, diff, -jnp.inf))
    q_decay = q * jnp.exp(cum)[..., None]
    k_decay = k * jnp.exp(cum[..., -1:] - cum)[..., None]
    last_decay = jnp.exp(cum[..., -1])

    def step(state, xs):
        u0_c, w_c, qk_c, qd_c, kd_c, ld_c = xs
        u = u0_c - jnp.einsum("bhcd,bhdv->bhcv", w_c, state)
        o = jnp.einsum("bhcd,bhdv->bhcv", qd_c, state) + jnp.einsum("bhts,bhsv->bhtv", qk_c, u)
        state = ld_c[..., None, None] * state + jnp.einsum("bhcd,bhcv->bhdv", kd_c, u)
        return state, o

    n, b, h, _, dk = q.shape
    state0 = jnp.zeros((b, h, dk, dv), jnp.float32)
    _, o = lax.scan(step, state0, (u0, w, qk, q_decay, k_decay, last_decay))
    return o


def l2_normalize(t):
    return t * lax.rsqrt(jnp.sum(t * t, axis=-1, keepdims=True) + L2_EPS)


def gated_deltanet_mixer(x, w_in, conv_w, a_log, dt_bias, norm_w, w_out):
    b, s, _ = x.shape
    hproj = x @ w_in
    qkv, z, beta_logit, a = jnp.split(
        hproj, [B_CONV_CH, B_CONV_CH + B_VAL_WIDTH, B_CONV_CH + B_VAL_WIDTH + B_HEADS], axis=-1)
    qkv = jax.nn.silu(causal_depthwise_conv(qkv, conv_w).astype(jnp.float32))
    q, k, v = jnp.split(qkv, [B_KEY_WIDTH, 2 * B_KEY_WIDTH], axis=-1)
    q = l2_normalize(q.reshape(b, s, B_HEADS, B_DK)) * (B_DK ** -0.5)
    k = l2_normalize(k.reshape(b, s, B_HEADS, B_DK))
    v = v.reshape(b, s, B_HEADS, B_DV)
    beta = jax.nn.sigmoid(beta_logit.astype(jnp.float32))
    g = -jnp.exp(a_log.astype(jnp.float32)) * jax.nn.softplus(
        a.astype(jnp.float32) + dt_bias.astype(jnp.float32))
    o = gated_delta_chunk(to_chunks(q), to_chunks(k), to_chunks(v),
                          to_chunks(beta[..., None])[..., 0], to_chunks(g[..., None])[..., 0])
    y = gated_rmsnorm(from_chunks(o), z, norm_w)
    return y.astype(x.dtype) @ w_out


def setup_inputs(seed: int = 0) -> dict:
    key = jax.random.key(seed)
    ks = jax.random.split(key, 16)
    f32 = jnp.float32
    D = D_MODEL
    x = jax.random.normal(ks[0], (BATCH, SEQ, D), f32)
    a_w_in = jax.random.normal(ks[1], (N_LAYERS_A, D, A_IN_WIDTH), f32) * (D ** -0.5)
    a_lower_bounds = jax.random.normal(ks[2], (DEPTH + 1, A_KEY_WIDTH), f32) * 0.5
    a_norm_w = 1.0 + 0.02 * jax.random.normal(ks[3], (N_LAYERS_A, A_DV), f32)
    a_w_out = jax.random.normal(ks[4], (N_LAYERS_A, A_VAL_WIDTH, D), f32) * (A_VAL_WIDTH ** -0.5) * DEEPNORM_BETA
    b_w_in = jax.random.normal(ks[5], (N_LAYERS_B, D, B_IN_WIDTH), f32) * (D ** -0.5)
    b_conv_w = jax.random.normal(ks[6], (N_LAYERS_B, B_CONV_WIDTH, B_CONV_CH), f32) * (B_CONV_WIDTH ** -0.5)
    b_a_log = jnp.log(jax.random.uniform(ks[7], (N_LAYERS_B, B_HEADS), f32, 1.0, 16.0))
    dt = jnp.exp(jax.random.uniform(ks[8], (N_LAYERS_B, B_HEADS), f32, math.log(1e-3), math.log(1e-1)))
    b_dt_bias = dt + jnp.log(-jnp.expm1(-dt))
    b_norm_w = 1.0 + 0.02 * jax.random.normal(ks[9], (N_LAYERS_B, B_DV), f32)
    b_w_out = jax.random.normal(ks[10], (N_LAYERS_B, B_VAL_WIDTH, D), f32) * (B_VAL_WIDTH ** -0.5) * DEEPNORM_BETA
    ln_g = 1.0 + 0.02 * jax.random.normal(ks[11], (DEPTH, D), f32)
    ln_b = 0.02 * jax.random.normal(ks[12], (DEPTH, D), f32)
    return {"x": x, "a_w_in": a_w_in, "a_lower_bounds": a_lower_bounds, "a_norm_w": a_norm_w,
            "a_w_out": a_w_out, "b_w_in": b_w_in, "b_conv_w": b_conv_w, "b_a_log": b_a_log,
            "b_dt_bias": b_dt_bias, "b_norm_w": b_norm_w, "b_w_out": b_w_out,
            "ln_g": ln_g, "ln_b": ln_b}


def reference(x, a_w_in, a_lower_bounds, a_norm_w, a_w_out, b_w_in, b_conv_w, b_a_log,
              b_dt_bias, b_norm_w, b_w_out, ln_g, ln_b):
    lb_all = jnp.cumsum(jax.nn.softmax(a_lower_bounds.astype(jnp.float32), axis=0), axis=0)
    for i in range(DEPTH):
        j = i // N_MIXERS
        if i % N_MIXERS == 0:
            y = hgrn2_mixer(x, a_w_in[j], lb_all[i], a_norm_w[j], a_w_out[j])
        else:
            y = gated_deltanet_mixer(x, b_w_in[j], b_conv_w[j], b_a_log[j], b_dt_bias[j],
                                     b_norm_w[j], b_w_out[j])
        x = layer_norm(DEEPNORM_ALPHA * x + y, ln_g[i], ln_b[i])
    return x
```

```python
import contextlib
import numpy as np
import ml_dtypes
import concourse.bass as bass
import concourse.mybir as mybir
from concourse.bass_utils import run_bass_kernel_spmd

F32 = mybir.dt.float32
BF16 = mybir.dt.bfloat16
AF = mybir.ActivationFunctionType
ALU = mybir.AluOpType

NCORE = 8
D = 4096
KC = D // 128
NTOK = 8192
SEQ = 4096
TOKC = NTOK // NCORE
TT = 512
CH = 64
NCH = TT // CH
HPC = 4
ALPHA = 4.0 ** 0.25
LN_EPS = 1e-5
RMS_EPS = 1e-6
L2_EPS = 1e-6

ENGS = ["tensor", "vector", "scalar", "gpsimd", "sync"]
DBG = {"passes": 2, "batches": 2, "tiles": SEQ // TT, "stop": 99}


class StopBuild(Exception):
    pass


def stage(k):
    if DBG["stop"] < k:
        raise StopBuild()


class Res:
    __slots__ = ("name", "lw", "rd", "bank", "acc")

    def __init__(self, name, bank=None):
        self.name = name
        self.lw = None
        self.rd = {}
        self.bank = bank
        self.acc = {}


class Sched:
    def __init__(self, nc, stack):
        self.nc = nc
        self.stack = stack
        self.q = {e: [] for e in ENGS}
        self.cnt = {}
        self.sems = {}
        self.waited = {e: {} for e in ENGS}
        for e in ENGS:
            self._sem(e)

    def _sem(self, key):
        if key not in self.sems:
            self.sems[key] = self.stack.enter_context(self.nc.semaphore("s_" + key))
            self.cnt[key] = 0
        return self.sems[key]

    def _deps(self, reads, writes):
        deps = {}

        def add(p):
            if p is None:
                return
            k, v = p
            if deps.get(k, 0) < v:
                deps[k] = v
        for r in reads:
            add(r.lw)
        for w in writes:
            add(w.lw)
            for k, v in w.rd.items():
                add((k, v))
        return deps

    def _waits(self, eng, deps, skip_self=False):
        waits = []
        wd = self.waited[eng]
        for k, v in deps.items():
            if skip_self and k == eng:
                continue
            if wd.get(k, 0) >= v:
                continue
            wd[k] = v
            waits.append((self.sems[k], v))
        return waits

    def _mark(self, reads, writes, key, val):
        for r in reads:
            if r.rd.get(key, 0) < val:
                r.rd[key] = val
        for w in writes:
            w.lw = (key, val)
            w.rd = {}

    def op(self, eng, fn, reads=(), writes=()):
        deps = self._deps(reads, writes)
        banks = []
        for r in list(reads) + list(writes):
            if r.bank is not None and r.bank not in banks:
                banks.append(r.bank)
        for b in banks:
            for k, v in b.acc.items():
                if k != eng and deps.get(k, 0) < v:
                    deps[k] = v
        waits = self._waits(eng, deps, skip_self=(eng == "tensor"))
        self.cnt[eng] += 1
        val = self.cnt[eng]
        sem = self.sems[eng]
        for b in banks:
            b.acc[eng] = val

        def run(e, waits=waits, fn=fn, sem=sem):
            for s, v in waits:
                e.wait_ge(s, v)
            fn(e).then_inc(sem, 1)
        self.q[eng].append(run)
        self._mark(reads, writes, eng, val)

    def dma(self, eng, stream, fn, reads=(), writes=()):
        self._sem(stream)
        deps = self._deps(reads, writes)
        waits = self._waits(eng, deps)
        self.cnt[stream] += 16
        val = self.cnt[stream]
        sem = self.sems[stream]

        def run(e, waits=waits, fn=fn, sem=sem):
            for s, v in waits:
                e.wait_ge(s, v)
            fn(e).then_inc(sem, 16)
        self.q[eng].append(run)
        self._mark(reads, writes, stream, val)

    def drain(self, eng):
        deps = {k: v for k, v in self.cnt.items() if v > 0 and k != eng}
        waits = self._waits(eng, deps)

        def run(e, waits=waits):
            for s, v in waits:
                e.wait_ge(s, v)
        self.q[eng].append(run)

    def emit(self):
        q = self.q
        self.q = {e: [] for e in ENGS}
        with self.nc.Block() as block:
            @block.tensor
            def _(e):
                for f in q["tensor"]:
                    f(e)

            @block.vector
            def _(e):
                for f in q["vector"]:
                    f(e)

            @block.scalar
            def _(e):
                for f in q["scalar"]:
                    f(e)

            @block.gpsimd
            def _(e):
                for f in q["gpsimd"]:
                    f(e)

            @block.sync
            def _(e):
                for f in q["sync"]:
                    f(e)


class Ctx:
    def __init__(self, nc, stack):
        self.nc = nc
        self.stack = stack
        self.S = Sched(nc, stack)
        self.outs = []

    def sb(self, name, shape, dt):
        t = self.stack.enter_context(self.nc.sbuf_tensor(name, list(shape), dt))
        return t, Res(name)

    def ps(self, name, shape, dt=F32):
        t = self.stack.enter_context(self.nc.psum_tensor(name, list(shape), dt))
        r = Res(name)
        r.bank = r
        return t, r


C_ID, C_ONES, C_CMASK, C_MINCL, C_MSTRICT, C_TRI, C_MSTRICT2, C_W = 0, 128, 256, 768, 832, 896, 960, 1024


def host_consts():
    c = np.zeros((128, C_W), np.float32)
    c[:, C_ID:C_ID + 128] = np.eye(128, dtype=np.float32)
    c[:, C_ONES:C_ONES + 128] = 1.0
    cm = np.ones(TT, np.float32)
    cm[::CH] = 0.0
    c[:, C_CMASK:C_CMASK + TT] = cm[None, :]
    i = np.arange(64)
    c[:64, C_MINCL:C_MINCL + 64] = (i[:, None] <= i[None, :]).astype(np.float32)
    c[:64, C_MSTRICT:C_MSTRICT + 64] = (i[:, None] < i[None, :]).astype(np.float32)
    c[:64, C_TRI:C_TRI + 64] = (i[:, None] <= i[None, :]).astype(np.float32)
    c[:64, C_MSTRICT2:C_MSTRICT2 + 64] = (i[:, None] > i[None, :]).astype(np.float32)
    return c


def load_consts(cx, cst_ap):
    S, nc = cx.S, cx.nc
    cst, Rc = cx.sb("cst_sb", [128, C_W], F32)
    S.dma("sync", "ld_cst", lambda e: e.dma_start(out=cst[:], in_=cst_ap), writes=[Rc])
    idb, Ridb = cx.sb("ident_bf", [128, 128], BF16)
    S.op("vector", lambda e: e.tensor_copy(out=idb[:], in_=cst[:, C_ID:C_ID + 128]), reads=[Rc], writes=[Ridb])
    cx.cst, cx.Rc, cx.idb, cx.Ridb = cst, Rc, idb, Ridb


def phase_a0(cx, xT, x_is_f32, win, lbraw, normw, yT, Ry):
    S, nc = cx.S, cx.nc
    cst, Rc, idb, Ridb = cx.cst, cx.Rc, cx.idb, cx.Ridb
    with contextlib.ExitStack() as st:
        old_stack = cx.stack
        cx.stack = st
        W, RW = cx.sb("a0_W", [128, KC, 2 * 4 * 128], BF16)
        X = [cx.sb(f"a0_X{i}", [128, KC, TT], BF16) for i in range(2)]
        lbr, Rlbr = cx.sb("a0_lbr", [128, 3, HPC], F32)
        lbe, Rlbe = cx.sb("a0_lbe", [128, 3, HPC], F32)
        lbs, Rlbs = cx.sb("a0_lbs", [128, HPC], F32)
        lb, Rlb = cx.sb("a0_lb", [128, HPC], F32)
        oml, Roml = cx.sb("a0_oml", [128, HPC], F32)
        noml, Rnoml = cx.sb("a0_noml", [128, HPC], F32)
        nw, Rnw = cx.sb("a0_nw", [128, 1], F32)
        S.dma("sync", "ld_s1", lambda e: e.dma_start(out=lbr[:], in_=lbraw.rearrange("r h p -> p r h"),
                                                      allow_slow_non_contiguous=True), writes=[Rlbr])
        S.dma("sync", "ld_s2", lambda e: e.dma_start(out=nw[:], in_=normw.rearrange("(p o) -> p o", o=1),
                                                      allow_slow_non_contiguous=True), writes=[Rnw])
        S.op("scalar", lambda e: e.activation(out=lbe[:], in_=lbr[:], func=AF.Exp), reads=[Rlbr], writes=[Rlbe])
        S.op("vector", lambda e: e.tensor_tensor(out=lbs[:], in0=lbe[:, 0, :], in1=lbe[:, 1, :], op=ALU.add),
             reads=[Rlbe], writes=[Rlbs])
        S.op("vector", lambda e: e.tensor_tensor(out=lbs[:], in0=lbs[:], in1=lbe[:, 2, :], op=ALU.add),
             reads=[Rlbe, Rlbs], writes=[Rlbs])
        S.op("vector", lambda e: e.reciprocal(out=lbs[:], in_=lbs[:]), reads=[Rlbs], writes=[Rlbs])
        S.op("vector", lambda e: e.tensor_tensor(out=lb[:], in0=lbe[:, 0, :], in1=lbs[:], op=ALU.mult),
             reads=[Rlbe, Rlbs], writes=[Rlb])
        S.op("vector", lambda e: e.tensor_scalar(out=oml[:], in0=lb[:], scalar1=-1.0, scalar2=1.0,
                                                 op0=ALU.mult, op1=ALU.add), reads=[Rlb], writes=[Roml])
        S.op("vector", lambda e: e.tensor_scalar(out=noml[:], in0=lb[:], scalar1=1.0, scalar2=-1.0,
                                                 op0=ALU.mult, op1=ALU.add), reads=[Rlb], writes=[Rnoml])
        Rsmall = [Rlb, Roml, Rnoml, Rnw]

        Pin = [cx.ps(f"a0_Pin{i}", [128, TT]) for i in range(3)]
        Patt, RPatt = cx.ps("a0_Patt", [128, TT])
        Pvt, RPvt = cx.ps("a0_Pvt", [64, NCH, 128], BF16)
        Pkt, RPkt = cx.ps("a0_Pkt", [64, NCH, 128], BF16)
        Po, RPo = cx.ps("a0_Po", [128, TT])
        PS_, RPS = cx.ps("a0_PS", [128, 4, 128])
        RPSs = [Res(f"a0_PS{i}", bank=RPS) for i in range(4)]

        def mkset(i):
            d = {}
            for nm in "ABCDEF":
                d[nm] = cx.sb(f"a0_{nm}{i}", [128, TT], F32)
            for nm in ["G1", "G2", "G3", "G4"]:
                d[nm] = cx.sb(f"a0_{nm}{i}", [128, TT], BF16)
            d["VT"] = cx.sb(f"a0_VT{i}", [64, NCH, 128], BF16)
            d["KT"] = cx.sb(f"a0_KT{i}", [64, NCH, 128], BF16)
            d["AT"] = cx.sb(f"a0_AT{i}", [64, NCH, CH], BF16)
            return d
        sets = [mkset(0), mkset(1)]
        Sf = [cx.sb(f"a0_Sf{i}", [128, 128], F32) for i in range(2)]
        Sb = [cx.sb(f"a0_Sb{i}", [128, 128], BF16) for i in range(2)]

        pin_i = [0]
        ps_i = [0]
        set_i = [0]
        xbuf_i = [0]

        def mm_group(out_ap, wcol, xb):
            def fn(e):
                ins = None
                for kc in range(KC):
                    ins = e.matmul(out_ap, W[:, kc, wcol:wcol + 128], X[xb][0][:, kc, :],
                                   start=(kc == 0), stop=(kc == KC - 1))
                return ins
            return fn

        for p in range(2):
            wv = win.rearrange("(kc q) h s m -> q kc (h s m)", q=128)
            for g in range(4):
                S.dma("gpsimd", "ld_W", lambda e, g=g, p=p: e.dma_start(
                    out=W[:, g * 8:(g + 1) * 8, :],
                    in_=wv[:, g * 8:(g + 1) * 8, p * 1024:(p + 1) * 1024]), writes=[RW])
            for b in range(2):
                for hh in range(2):
                    S.op("vector", lambda e, hh=hh: e.memset(Sf[hh][0][:], 0.0), writes=[Sf[hh][1]])
                    S.op("vector", lambda e, hh=hh: e.memset(Sb[hh][0][:], 0.0), writes=[Sb[hh][1]])
                for n in range(SEQ // TT):
                    gtok = b * SEQ + n * TT
                    shard, off = gtok // TOKC, gtok % TOKC
                    xb = xbuf_i[0] % 2
                    xbuf_i[0] += 1
                    xsrc = xT[shard].rearrange("(kc q) t -> q kc t", q=128)[:, :, off:off + TT]
                    for g in range(4):
                        S.dma("gpsimd" if x_is_f32 else "sync", f"ld_X{xb}",
                              lambda e, g=g, xb=xb, xsrc=xsrc: e.dma_start(
                                  out=X[xb][0][:, g * 8:(g + 1) * 8, :], in_=xsrc[:, g * 8:(g + 1) * 8, :]),
                              writes=[X[xb][1]])
                    for hh in range(2):
                        hl = p * 2 + hh
                        T = sets[set_i[0] % 2]
                        set_i[0] += 1
                        A, B, C, Dt, E, Fz = T["A"], T["B"], T["C"], T["D"], T["E"], T["F"]
                        G1, G2, G3, G4 = T["G1"], T["G2"], T["G3"], T["G4"]
                        VT, KT, AT = T["VT"], T["KT"], T["AT"]
                        lbc, omlc, nomlc = lb[:, hl:hl + 1], oml[:, hl:hl + 1], noml[:, hl:hl + 1]
                        rdW = [RW, X[xb][1]]

                        def nextpin():
                            i = pin_i[0] % 3
                            pin_i[0] += 1
                            return Pin[i]
                        Pq, RPq = nextpin()
                        S.op("tensor", mm_group(Pq[:], (hh * 4 + 0) * 128, xb), reads=rdW, writes=[RPq])
                        S.op("scalar", lambda e, Pq=Pq, A=A: e.activation(out=A[0][:], in_=Pq[:], func=AF.Silu),
                             reads=[RPq], writes=[A[1]])
                        Pf, RPf = nextpin()
                        S.op("tensor", mm_group(Pf[:], (hh * 4 + 1) * 128, xb), reads=rdW, writes=[RPf])
                        S.op("scalar", lambda e, Pf=Pf, B=B: e.activation(out=B[0][:], in_=Pf[:], func=AF.Sigmoid),
                             reads=[RPf], writes=[B[1]])
                        Pi, RPi = nextpin()
                        S.op("tensor", mm_group(Pi[:], (hh * 4 + 2) * 128, xb), reads=rdW, writes=[RPi])
                        S.op("scalar", lambda e, Pi=Pi, G4=G4: e.copy(out=G4[0][:], in_=Pi[:]),
                             reads=[RPi], writes=[G4[1]])
                        Pz, RPz = nextpin()
                        S.op("tensor", mm_group(Pz[:], (hh * 4 + 3) * 128, xb), reads=rdW, writes=[RPz])
                        S.op("scalar", lambda e, Pz=Pz, Fz=Fz: e.activation(out=Fz[0][:], in_=Pz[:], func=AF.Silu),
                             reads=[RPz], writes=[Fz[1]])
                        S.op("scalar", lambda e, B=B, C=C, omlc=omlc, lbc=lbc: e.activation(
                            out=C[0][:], in_=B[0][:], func=AF.Ln, scale=omlc, bias=lbc),
                            reads=[B[1]] + Rsmall, writes=[C[1]])
                        S.op("vector", lambda e, B=B, omlc=omlc, nomlc=nomlc: e.tensor_scalar(
                            out=B[0][:], in0=B[0][:], scalar1=nomlc, scalar2=omlc, op0=ALU.mult, op1=ALU.add),
                            reads=[B[1]] + Rsmall, writes=[B[1]])
                        S.op("vector", lambda e, C=C, Dt=Dt: e.tensor_tensor_scan(
                            out=Dt[0][:], data0=cst[:, C_CMASK:C_CMASK + TT], data1=C[0][:], initial=0.0,
                            op0=ALU.mult, op1=ALU.add), reads=[C[1], Rc], writes=[Dt[1]])
                        S.op("scalar", lambda e, C=C, Dt=Dt: e.activation(out=C[0][:], in_=Dt[0][:], func=AF.Exp),
                             reads=[Dt[1]], writes=[C[1]])
                        S.op("scalar", lambda e, E=E, Dt=Dt: e.activation(out=E[0][:], in_=Dt[0][:], func=AF.Exp,
                                                                          scale=-1.0),
                             reads=[Dt[1]], writes=[E[1]])
                        S.op("vector", lambda e, A=A, C=C, G3=G3: e.tensor_tensor(
                            out=G3[0][:], in0=A[0][:], in1=C[0][:], op=ALU.mult), reads=[A[1], C[1]], writes=[G3[1]])
                        S.op("vector", lambda e, B=B, E=E: e.tensor_tensor(
                            out=E[0][:], in0=B[0][:], in1=E[0][:], op=ALU.mult), reads=[B[1], E[1]], writes=[E[1]])
                        S.op("gpsimd", lambda e, E=E, G1=G1: e.tensor_copy(out=G1[0][:], in_=E[0][:]),
                             reads=[E[1]], writes=[G1[1]])
                        S.op("vector", lambda e, E=E, C=C, G2=G2: e.tensor_tensor(
                            out=G2[0][:].rearrange("p (c t) -> p c t", t=CH),
                            in0=E[0][:].rearrange("p (c t) -> p c t", t=CH),
                            in1=C[0][:].rearrange("p (c t) -> p c t", t=CH)[:, :, CH - 1:CH].to_broadcast([128, NCH, CH]),
                            op=ALU.mult), reads=[E[1], C[1]], writes=[G2[1]])
                        def tr_fn(dst, src):
                            def fn(e):
                                ins = None
                                for c in range(NCH):
                                    ins = e.transpose(dst[:, c, :], src[:, c * CH:(c + 1) * CH], idb[:])
                                return ins
                            return fn
                        S.op("tensor", tr_fn(Pvt, G4[0]), reads=[G4[1], Ridb], writes=[RPvt])
                        S.op("scalar", lambda e, VT=VT: e.copy(out=VT[0][:], in_=Pvt[:]), reads=[RPvt], writes=[VT[1]])
                        S.op("tensor", tr_fn(Pkt, G2[0]), reads=[G2[1], Ridb], writes=[RPkt])
                        S.op("vector", lambda e, KT=KT: e.tensor_copy(out=KT[0][:], in_=Pkt[:]),
                             reads=[RPkt], writes=[KT[1]])
                        def att_fn(G1=G1, G3=G3):
                            def fn(e):
                                ins = None
                                for c in range(NCH):
                                    ins = e.matmul(Patt[0:64, c * CH:(c + 1) * CH], G1[0][:, c * CH:(c + 1) * CH],
                                                   G3[0][:, c * CH:(c + 1) * CH], start=True, stop=True)
                                return ins
                            return fn
                        S.op("tensor", att_fn(), reads=[G1[1], G3[1]], writes=[RPatt])
                        S.op("vector", lambda e, AT=AT: e.tensor_tensor(
                            out=AT[0][:], in0=Patt[0:64, :].rearrange("p (c t) -> p c t", t=CH),
                            in1=cst[0:64, C_MINCL:C_MINCL + 64].unsqueeze(1).to_broadcast([64, NCH, CH]),
                            op=ALU.mult), reads=[RPatt, Rc], writes=[AT[1]])
                        for c in range(NCH):
                            def o_fn(e, c=c, VT=VT, AT=AT, G3=G3, hh=hh):
                                e.matmul(Po[:, c * CH:(c + 1) * CH], VT[0][:, c, :], AT[0][:, c, :],
                                         start=True, stop=False)
                                return e.matmul(Po[:, c * CH:(c + 1) * CH], Sb[hh][0][:], G3[0][:, c * CH:(c + 1) * CH],
                                                start=False, stop=True)
                            S.op("tensor", o_fn, reads=[VT[1], AT[1], G3[1], Sb[hh][1]], writes=[RPo])
                            si = ps_i[0] % 4
                            ps_i[0] += 1
                            S.op("tensor", lambda e, c=c, KT=KT, VT=VT, si=si: e.matmul(
                                PS_[:, si, :], KT[0][:, c, :], VT[0][:, c, :], start=True, stop=True),
                                reads=[KT[1], VT[1]], writes=[RPSs[si]])
                            S.op("vector", lambda e, c=c, C=C, si=si, hh=hh: e.scalar_tensor_tensor(
                                out=Sf[hh][0][:], in0=Sf[hh][0][:], scalar=C[0][:, c * CH + CH - 1:c * CH + CH],
                                in1=PS_[:, si, :], op0=ALU.mult, op1=ALU.add),
                                reads=[Sf[hh][1], C[1], RPSs[si]], writes=[Sf[hh][1]])
                            S.op("scalar", lambda e, hh=hh: e.copy(out=Sb[hh][0][:], in_=Sf[hh][0][:]),
                                 reads=[Sf[hh][1]], writes=[Sb[hh][1]])
                        S.op("scalar", lambda e, Dt=Dt: e.copy(out=Dt[0][:], in_=Po[:]), reads=[RPo], writes=[Dt[1]])
                        S.op("scalar", lambda e, E=E: e.activation(out=E[0][:], in_=Po[:], func=AF.Square),
                             reads=[RPo], writes=[E[1]])
                        S.op("tensor", lambda e, E=E: e.matmul(Patt[:], cst[:, C_ONES:C_ONES + 128], E[0][:],
                                                               start=True, stop=True),
                             reads=[E[1], Rc], writes=[RPatt])
                        S.op("scalar", lambda e, C=C: e.activation(out=C[0][:], in_=Patt[:], func=AF.Ln,
                                                                  scale=1.0 / 128.0, bias=cx.eps_rms[:, 0:1]),
                             reads=[RPatt, C[1], cx.Reps], writes=[C[1]])
                        S.op("scalar", lambda e, C=C: e.activation(out=C[0][:], in_=C[0][:], func=AF.Exp, scale=-0.5),
                             reads=[C[1]], writes=[C[1]])
                        S.op("vector", lambda e, Dt=Dt, C=C: e.tensor_tensor(out=Dt[0][:], in0=Dt[0][:], in1=C[0][:],
                                                                            op=ALU.mult),
                             reads=[Dt[1], C[1]], writes=[Dt[1]])
                        S.op("vector", lambda e, Dt=Dt, Fz=Fz, G1=G1: e.scalar_tensor_tensor(
                            out=G1[0][:], in0=Dt[0][:], scalar=nw[:, 0:1], in1=Fz[0][:], op0=ALU.mult, op1=ALU.mult),
                            reads=[Dt[1], Fz[1], Rnw, G1[1]], writes=[G1[1]])
                        S.dma("sync", f"st_y{(set_i[0] - 1) % 2}", lambda e, G1=G1, shard=shard, hl=hl, off=off: e.dma_start(
                            out=yT[shard, hl * 128:(hl + 1) * 128, off:off + TT], in_=G1[0][:]),
                            reads=[G1[1]], writes=[Ry])
        for en in ENGS:
            S.drain(en)
        S.emit()
        cx.stack = old_stack


def phase_b(cx, yT_own, wout, xres, lng, lnb, zs, o32, obf):
    S, nc = cx.S, cx.nc
    cst, Rc = cx.cst, cx.Rc
    NG = 8
    with contextlib.ExitStack() as st:
        old_stack = cx.stack
        cx.stack = st
        Y, RY = cx.sb("b_Y", [128, KC, TOKC], BF16)
        WO = [cx.sb(f"b_WO{i}", [128, KC, 512], BF16) for i in range(2)]
        XR = [cx.sb(f"b_XR{i}", [128, TOKC], F32) for i in range(2)]
        Z = [cx.sb(f"b_Z{i}", [128, TOKC], F32) for i in range(2)]
        Z2 = [cx.sb(f"b_Z2{i}", [128, TOKC], F32) for i in range(2)]
        gt, Rg = cx.sb("b_g", [128, KC], F32)
        bt, Rb = cx.sb("b_b", [128, KC], F32)
        MEAN, RM = cx.sb("b_mean", [128, TOKC], F32)
        RSTD, RR = cx.sb("b_rstd", [128, TOKC], F32)
        NMR, RN = cx.sb("b_nmr", [128, TOKC], F32)
        OB = [cx.sb(f"b_OB{i}", [128, TOKC], BF16) for i in range(2)]
        Pp = [cx.ps(f"b_Pp{i}", [128, 512]) for i in range(4)]
        Psum = [cx.ps(f"b_Psum{i}", [128, 512]) for i in range(2)]
        Psq = [cx.ps(f"b_Psq{i}", [128, 512]) for i in range(2)]
        Rzs = [Res(f"zs{i}") for i in range(KC)]
        Rout = Res("b_out")

        S.dma("sync", "ld_s3", lambda e: e.dma_start(out=gt[:], in_=lng.rearrange("(c p) -> p c", p=128),
                                                      allow_slow_non_contiguous=True), writes=[Rg])
        S.dma("sync", "ld_s4", lambda e: e.dma_start(out=bt[:], in_=lnb.rearrange("(c p) -> p c", p=128),
                                                      allow_slow_non_contiguous=True), writes=[Rb])
        yv = yT_own.rearrange("(kc q) t -> q kc t", q=128)
        for g in range(4):
            S.dma("sync", "ld_Y", lambda e, g=g: e.dma_start(out=Y[:, g * 8:(g + 1) * 8, :],
                                                            in_=yv[:, g * 8:(g + 1) * 8, :]), writes=[RY])
        wv = wout.rearrange("(kc q) n -> q kc n", q=128)
        pp_i = 0
        for fo in range(KC):
            g, j = fo // 4, fo % 4
            wb = g % 2
            if j == 0:
                for gg in range(4):
                    S.dma("gpsimd", f"ld_WO{wb}", lambda e, gg=gg, g=g, wb=wb: e.dma_start(
                        out=WO[wb][0][:, gg * 8:(gg + 1) * 8, :],
                        in_=wv[:, gg * 8:(gg + 1) * 8, g * 512:(g + 1) * 512]), writes=[WO[wb][1]])
            xb = fo % 2
            S.dma("sync", f"ld_XR{xb}", lambda e, fo=fo, xb=xb: e.dma_start(
                out=XR[xb][0][:], in_=xres[fo * 128:(fo + 1) * 128, :]), writes=[XR[xb][1]])
            for half in range(2):
                P, RP = Pp[pp_i % 4]
                pp_i += 1

                def mm(e, P=P, wb=wb, j=j, half=half):
                    ins = None
                    for kc in range(KC):
                        ins = e.matmul(P[:], WO[wb][0][:, kc, j * 128:(j + 1) * 128],
                                       Y[:, kc, half * 512:(half + 1) * 512], start=(kc == 0), stop=(kc == KC - 1))
                    return ins
                S.op("tensor", mm, reads=[WO[wb][1], RY], writes=[RP])
                S.op("vector", lambda e, P=P, xb=xb, half=half: e.scalar_tensor_tensor(
                    out=Z[xb][0][:, half * 512:(half + 1) * 512], in0=XR[xb][0][:, half * 512:(half + 1) * 512],
                    scalar=ALPHA, in1=P[:], op0=ALU.mult, op1=ALU.add),
                    reads=[XR[xb][1], RP], writes=[Z[xb][1]])
            S.op("scalar", lambda e, xb=xb: e.activation(out=Z2[xb][0][:], in_=Z[xb][0][:], func=AF.Square),
                 reads=[Z[xb][1]], writes=[Z2[xb][1]])
            for half in range(2):
                S.op("tensor", lambda e, xb=xb, half=half, fo=fo: e.matmul(
                    Psum[half][0][:], cst[:, C_ONES:C_ONES + 128], Z[xb][0][:, half * 512:(half + 1) * 512],
                    start=(fo == 0), stop=(fo == KC - 1)), reads=[Z[xb][1], Rc], writes=[Psum[half][1]])
                S.op("tensor", lambda e, xb=xb, half=half, fo=fo: e.matmul(
                    Psq[half][0][:], cst[:, C_ONES:C_ONES + 128], Z2[xb][0][:, half * 512:(half + 1) * 512],
                    start=(fo == 0), stop=(fo == KC - 1)), reads=[Z2[xb][1], Rc], writes=[Psq[half][1]])
            S.dma("sync", f"st_Z{xb}", lambda e, fo=fo, xb=xb: e.dma_start(
                out=zs[fo * 128:(fo + 1) * 128, :], in_=Z[xb][0][:]), reads=[Z[xb][1]], writes=[Rzs[fo]])
        for half in range(2):
            sl = slice(half * 512, (half + 1) * 512)
            S.op("scalar", lambda e, half=half, sl=sl: e.activation(
                out=MEAN[:, sl], in_=Psum[half][0][:], func=AF.Copy, scale=1.0 / D),
                reads=[Psum[half][1]], writes=[RM])
            S.op("vector", lambda e, sl=sl: e.tensor_tensor(out=NMR[:, sl], in0=MEAN[:, sl], in1=MEAN[:, sl],
                                                            op=ALU.mult), reads=[RM], writes=[RN])
            S.op("vector", lambda e, half=half, sl=sl: e.scalar_tensor_tensor(
                out=RSTD[:, sl], in0=Psq[half][0][:], scalar=1.0 / D, in1=NMR[:, sl], op0=ALU.mult, op1=ALU.subtract),
                reads=[Psq[half][1], RN], writes=[RR])
            S.op("scalar", lambda e, sl=sl: e.activation(out=RSTD[:, sl], in_=RSTD[:, sl], func=AF.Ln,
                                                         bias=cx.eps_rms[:, 1:2]),
                 reads=[RR, cx.Reps], writes=[RR])
            S.op("scalar", lambda e, sl=sl: e.activation(out=RSTD[:, sl], in_=RSTD[:, sl], func=AF.Exp, scale=-0.5),
                 reads=[RR], writes=[RR])
            S.op("vector", lambda e, sl=sl: e.scalar_tensor_tensor(
                out=NMR[:, sl], in0=MEAN[:, sl], scalar=-1.0, in1=RSTD[:, sl], op0=ALU.mult, op1=ALU.mult),
                reads=[RM, RR], writes=[RN])
        for fo in range(KC):
            xb = fo % 2
            S.dma("sync", f"ld_Zb{xb}", lambda e, fo=fo, xb=xb: e.dma_start(
                out=Z[xb][0][:], in_=zs[fo * 128:(fo + 1) * 128, :]), reads=[Rzs[fo]], writes=[Z[xb][1]])
            S.op("vector", lambda e, xb=xb: e.tensor_tensor(out=Z[xb][0][:], in0=Z[xb][0][:], in1=RSTD[:],
                                                            op=ALU.mult), reads=[Z[xb][1], RR], writes=[Z[xb][1]])
            S.op("vector", lambda e, xb=xb: e.tensor_tensor(out=Z[xb][0][:], in0=Z[xb][0][:], in1=NMR[:],
                                                            op=ALU.add), reads=[Z[xb][1], RN], writes=[Z[xb][1]])
            S.op("scalar", lambda e, xb=xb, fo=fo: e.activation(
                out=Z2[xb][0][:], in_=Z[xb][0][:], func=AF.Identity, scale=gt[:, fo:fo + 1], bias=bt[:, fo:fo + 1]),
                reads=[Z[xb][1], Rg, Rb], writes=[Z2[xb][1]])
            S.dma("sync", f"st_O{xb}", lambda e, fo=fo, xb=xb: e.dma_start(
                out=o32[fo * 128:(fo + 1) * 128, :], in_=Z2[xb][0][:]), reads=[Z2[xb][1]], writes=[Rout])
            if obf is not None:
                S.op("gpsimd", lambda e, xb=xb: e.tensor_copy(out=OB[xb][0][:], in_=Z2[xb][0][:]),
                     reads=[Z2[xb][1]], writes=[OB[xb][1]])
                S.dma("sync", f"st_OB{xb}", lambda e, fo=fo, xb=xb: e.dma_start(
                    out=obf[fo * 128:(fo + 1) * 128, :], in_=OB[xb][0][:]), reads=[OB[xb][1]], writes=[Rout])
        for en in ENGS:
            S.drain(en)
        S.emit()
        cx.stack = old_stack


def phase_a1(cx, xT, win, wba, convw, hv, normw, yT):
    S, nc = cx.S, cx.nc
    cst, Rc, idb, Ridb = cx.cst, cx.Rc, cx.idb, cx.Ridb
    NT_B = SEQ // TT
    with contextlib.ExitStack() as st:
        old_stack = cx.stack
        cx.stack = st
        W, RW = cx.sb("a1_W", [128, KC, 2 * 4 * 128], BF16)
        X = [cx.sb(f"a1_X{i}", [128, KC, TT], BF16) for i in range(2)]
        Wba, RWba = cx.sb("a1_Wba", [128, KC, 8], BF16)
        cw, Rcw = cx.sb("a1_cw", [128, 3, HPC, 4], F32)
        hvt, Rhv = cx.sb("a1_hv", [64, 2, HPC], F32)
        nexpa, Rnexpa = cx.sb("a1_nexpa", [64, HPC], F32)
        nw, Rnw = cx.sb("a1_nw", [128, 1], F32)
        lnsc, Rlnsc = cx.sb("a1_lnsc", [128, 1], F32)
        BAst, RBAst = cx.sb("a1_BAst", [64, 2 * NT_B, NCH, 8], F32)
        BA, RBA = cx.sb("a1_BA", [8, TT], F32)
        S.dma("gpsimd", "ld_s5", lambda e: e.dma_start(out=Wba[:], in_=wba), writes=[RWba])
        S.dma("sync", "ld_s6", lambda e: e.dma_start(out=cw[:], in_=convw), writes=[Rcw])
        S.dma("sync", "ld_s7", lambda e: e.dma_start(out=hvt[:], in_=hv), writes=[Rhv])
        S.dma("sync", "ld_s8", lambda e: e.dma_start(out=nw[:], in_=normw.rearrange("(p o) -> p o", o=1),
                                                       allow_slow_non_contiguous=True), writes=[Rnw])
        S.op("scalar", lambda e: e.activation(out=nexpa[:], in_=hvt[:, 0, :], func=AF.Exp), reads=[Rhv], writes=[Rnexpa])
        S.op("vector", lambda e: e.tensor_scalar(out=nexpa[:], in0=nexpa[:], scalar1=-1.0, scalar2=None, op0=ALU.mult),
             reads=[Rnexpa], writes=[Rnexpa])
        import math
        S.op("vector", lambda e: e.memset(lnsc[:], math.log(128.0 ** -0.5)), writes=[Rlnsc])

        Pin = [cx.ps(f"a1_Pin{i}", [128, TT]) for i in range(2)]
        PM, RPM = cx.ps("a1_PM", [128, TT])
        Pcb, RPcb = cx.ps("a1_Pcb", [128, TT])
        Ptr, RPtr = cx.ps("a1_Ptr", [64, NCH, 128], BF16)
        P5, RP5 = cx.ps("a1_P5", [128, TT])
        P6, RP6 = cx.ps("a1_P6", [128, TT])
        Po, RPo = cx.ps("a1_Po", [128, TT])

        GT, RGT = cx.sb("a1_GT", [64, 2 * NCH, CH], F32)
        DCT, RDCT = cx.sb("a1_DCT", [64, 2 * NCH, CH], F32)
        ECB, RECB = cx.sb("a1_ECB", [128, 2 * NCH, CH], F32)
        sm = {}
        for nm in ["beta", "g", "t1", "cum", "ecum", "ekd", "nbe", "nbeta"]:
            sm[nm] = cx.sb(f"a1_s_{nm}", [64, 2, NCH], F32)
        ELAST, RELAST = cx.sb("a1_elast", [128, 2, NCH], F32)
        LASTB, RLASTB = cx.sb("a1_lastb", [64, 2, NCH], F32)
        RAW = [[cx.sb(f"a1_RAW{hh}{s_}", [128, 3 + TT], F32) for s_ in range(3)] for hh in range(2)]
        QS = cx.sb("a1_QS", [128, TT], F32)
        KS = cx.sb("a1_KS", [128, TT], F32)
        CV = cx.sb("a1_CV", [128, TT], F32)
        VSb = cx.sb("a1_VS", [128, TT], BF16)
        SQ = cx.sb("a1_SQ", [128, TT], F32)
        RS = cx.sb("a1_RS", [128, TT], F32)
        QN = cx.sb("a1_QN", [128, TT], F32)
        qTb = cx.sb("a1_qT", [128, TT], BF16)
        kTb = cx.sb("a1_kT", [128, TT], BF16)
        qdb = cx.sb("a1_qd", [128, TT], BF16)
        ZS = cx.sb("a1_ZS", [128, TT], F32)
        KD = cx.sb("a1_KD", [64, NCH, 128], BF16)
        KBE = cx.sb("a1_KBE", [64, NCH, 128], BF16)
        BV = cx.sb("a1_BV", [64, NCH, 128], BF16)
        Nm = [cx.sb(f"a1_N{i}", [64, NCH, CH], BF16) for i in range(2)]
        Bm = [cx.sb(f"a1_B{i}", [64, NCH, CH], BF16) for i in range(2)]
        Xm = [cx.sb(f"a1_Xm{i}", [64, NCH, CH], BF16) for i in range(2)]
        QKT = cx.sb("a1_QKT", [64, NCH, CH], BF16)
        WT = cx.sb("a1_WT", [128, TT], BF16)
        U = [cx.sb(f"a1_U{i}", [64, 128], BF16) for i in range(2)]
        Ot = cx.sb("a1_O", [128, TT], F32)
        Sf = [cx.sb(f"a1_Sf{i}", [128, 128], F32) for i in range(2)]
        Sb = [cx.sb(f"a1_Sb{i}", [128, 128], BF16) for i in range(2)]
        Yb = [cx.sb(f"a1_Yb{i}", [128, TT], BF16) for i in range(2)]

        ident8 = cst[0:8, C_ID:C_ID + 8]
        ones64 = cst[0:64, C_ONES:C_ONES + 128]
        tri = cst[0:64, C_TRI:C_TRI + 64]
        mincl = cst[0:64, C_MINCL:C_MINCL + 64]
        mstrict_ts = cst[0:64, C_MSTRICT2:C_MSTRICT2 + 64]

        pin_i = [0]
        xbuf_i = [0]
        y_i = [0]

        def mm_group(out_ap, wtile, xb, kcn=KC):
            def fn(e):
                ins = None
                for kc in range(KC):
                    ins = e.matmul(out_ap, wtile(kc), X[xb][0][:, kc, :], start=(kc == 0), stop=(kc == KC - 1))
                return ins
            return fn

        def v3(ap):
            return ap.rearrange("p (c t) -> p c t", t=CH)

        def _body():
            for p in range(DBG["passes"]):
                wv = win.rearrange("(kc q) h s m -> q kc (h s m)", q=128)
                for g in range(4):
                    S.dma("gpsimd", "ld_W", lambda e, g=g, p=p: e.dma_start(
                        out=W[:, g * 8:(g + 1) * 8, :],
                        in_=wv[:, g * 8:(g + 1) * 8, p * 1024:(p + 1) * 1024]), writes=[RW])
                for b in range(DBG["batches"]):
                    for hh in range(2):
                        S.op("vector", lambda e, hh=hh: e.memset(Sf[hh][0][:], 0.0), writes=[Sf[hh][1]])
                        S.op("vector", lambda e, hh=hh: e.memset(Sb[hh][0][:], 0.0), writes=[Sb[hh][1]])
                        for s_ in range(3):
                            S.op("gpsimd", lambda e, hh=hh, s_=s_: e.memset(RAW[hh][s_][0][:, TT:TT + 3], 0.0),
                                 writes=[RAW[hh][s_][1]])
                    for n in range(DBG["tiles"]):
                        ti = b * NT_B + n
                        gtok = b * SEQ + n * TT
                        shard, off = gtok // TOKC, gtok % TOKC
                        xb = xbuf_i[0] % 2
                        xbuf_i[0] += 1
                        xsrc = xT[shard].rearrange("(kc q) t -> q kc t", q=128)[:, :, off:off + TT]
                        for g in range(4):
                            S.dma("sync", f"ld_X{xb}", lambda e, g=g, xb=xb, xsrc=xsrc: e.dma_start(
                                out=X[xb][0][:, g * 8:(g + 1) * 8, :], in_=xsrc[:, g * 8:(g + 1) * 8, :]),
                                writes=[X[xb][1]])
                        if p == 0:
                            S.op("tensor", mm_group(PM[0:8, :], lambda kc: Wba[:, kc, :], xb), reads=[RWba, X[xb][1]],
                                 writes=[RPM])
                            S.op("scalar", lambda e: e.copy(out=BA[:], in_=PM[0:8, :]), reads=[RPM], writes=[RBA])

                            def bat(e):
                                ins = None
                                for c in range(NCH):
                                    ins = e.matmul(PM[0:64, c * 8:(c + 1) * 8], BA[0:8, c * CH:(c + 1) * CH], ident8,
                                                   start=True, stop=True)
                                return ins
                            S.op("tensor", bat, reads=[RBA, Rc], writes=[RPM])
                            S.op("vector", lambda e, ti=ti: e.tensor_copy(
                                out=BAst[:, ti, :, :], in_=PM[0:64, 0:NCH * 8].rearrange("p (c m) -> p c m", m=8)),
                                reads=[RPM], writes=[RBAst])
                        stage(2)
                        braw = BAst[:, ti, :, 2 * p:2 * p + 2].rearrange("p c h -> p h c")
                        araw = BAst[:, ti, :, 4 + 2 * p:4 + 2 * p + 2].rearrange("p c h -> p h c")
                        beta, g_, t1, cum, ecum, ekd, nbe, nbeta = [sm[k] for k in
                                                                   ["beta", "g", "t1", "cum", "ecum", "ekd", "nbe", "nbeta"]]
                        S.op("scalar", lambda e, braw=braw: e.activation(out=beta[0][:], in_=braw, func=AF.Sigmoid),
                             reads=[RBAst], writes=[beta[1]])
                        S.op("vector", lambda e, araw=araw, p=p: e.tensor_tensor(
                            out=t1[0][:], in0=araw, in1=hvt[:, 1, 2 * p:2 * p + 2].unsqueeze(2).to_broadcast([64, 2, NCH]),
                            op=ALU.add), reads=[RBAst, Rhv], writes=[t1[1]])
                        S.op("scalar", lambda e: e.activation(out=t1[0][:], in_=t1[0][:], func=AF.Exp),
                             reads=[t1[1]], writes=[t1[1]])
                        S.op("scalar", lambda e: e.activation(out=t1[0][:], in_=t1[0][:], func=AF.Ln,
                                                              bias=cx.eps_rms[0:64, 3:4]),
                             reads=[t1[1], cx.Reps], writes=[t1[1]])
                        S.op("vector", lambda e, p=p: e.tensor_tensor(
                            out=g_[0][:], in0=t1[0][:], in1=nexpa[:, 2 * p:2 * p + 2].unsqueeze(2).to_broadcast([64, 2, NCH]),
                            op=ALU.mult), reads=[t1[1], Rnexpa], writes=[g_[1]])
                        g2d = g_[0][:].rearrange("p h c -> p (h c)")
                        S.op("tensor", lambda e, g2d=g2d: e.matmul(PM[0:64, 0:16], tri, g2d, start=True, stop=True),
                             reads=[g_[1], Rc], writes=[RPM])
                        S.op("vector", lambda e: e.tensor_copy(out=cum[0][:].rearrange("p h c -> p (h c)"), in_=PM[0:64, 0:16]),
                             reads=[RPM], writes=[cum[1]])
                        S.op("tensor", lambda e, g2d=g2d: e.matmul(PM[:, 16:32], ones64, g2d, start=True, stop=True),
                             reads=[g_[1], Rc, cum[1]], writes=[RPM])
                        S.op("scalar", lambda e: e.activation(out=ELAST[:].rearrange("p h c -> p (h c)"), in_=PM[:, 16:32],
                                                              func=AF.Exp), reads=[RPM], writes=[RELAST])
                        S.op("vector", lambda e: e.tensor_tensor(
                            out=LASTB[:].rearrange("p h c -> p (h c)"), in0=PM[0:64, 16:32],
                            in1=cum[0][:].rearrange("p h c -> p (h c)"), op=ALU.subtract),
                            reads=[RPM, cum[1]], writes=[RLASTB])
                        S.op("scalar", lambda e: e.activation(out=ekd[0][:], in_=LASTB[:], func=AF.Exp),
                             reads=[RLASTB], writes=[ekd[1]])
                        S.op("scalar", lambda e: e.activation(out=ecum[0][:], in_=cum[0][:], func=AF.Exp),
                             reads=[cum[1]], writes=[ecum[1]])
                        S.op("vector", lambda e: e.scalar_tensor_tensor(
                            out=nbe[0][:], in0=beta[0][:], scalar=-1.0, in1=ecum[0][:], op0=ALU.mult, op1=ALU.mult),
                            reads=[beta[1], ecum[1]], writes=[nbe[1]])
                        S.op("vector", lambda e: e.tensor_scalar(out=nbeta[0][:], in0=beta[0][:], scalar1=-1.0, scalar2=None,
                                                                 op0=ALU.mult), reads=[beta[1]], writes=[nbeta[1]])
                        stage(3)
                        S.op("vector", lambda e, g2d=g2d: e.tensor_tensor(
                            out=GT[:], in0=tri.unsqueeze(1).to_broadcast([64, 2 * NCH, CH]),
                            in1=g2d.unsqueeze(2).to_broadcast([64, 2 * NCH, CH]), op=ALU.mult),
                            reads=[g_[1], Rc], writes=[RGT])
                        for hh in range(2):
                            def cbf(e, hh=hh):
                                ins = None
                                for c in range(NCH):
                                    ins = e.matmul(Pcb[:, c * CH:(c + 1) * CH], ones64, GT[:, hh * NCH + c, :],
                                                   start=True, stop=True)
                                return ins
                            S.op("tensor", cbf, reads=[RGT, Rc], writes=[RPcb])
                            S.op("scalar", lambda e, hh=hh: e.activation(
                                out=ECB[:, hh * NCH:(hh + 1) * NCH, :], in_=v3(Pcb[:]), func=AF.Exp),
                                reads=[RPcb], writes=[RECB])
                            S.op("vector", lambda e, hh=hh: e.tensor_tensor(
                                out=DCT[:, hh * NCH:(hh + 1) * NCH, :], in0=v3(Pcb[0:64, :]),
                                in1=cum[0][:, hh, :].unsqueeze(2).to_broadcast([64, NCH, CH]), op=ALU.subtract),
                                reads=[RPcb, cum[1]], writes=[RDCT])
                            S.op("vector", lambda e, hh=hh: e.tensor_tensor(
                                out=GT[:, hh * NCH:(hh + 1) * NCH, :],
                                in0=cum[0][:, hh, :].unsqueeze(2).to_broadcast([64, NCH, CH]),
                                in1=v3(Pcb[0:64, :]), op=ALU.subtract),
                                reads=[RPcb, cum[1]], writes=[RGT])
                        S.op("vector", lambda e: e.tensor_scalar(out=DCT[:], in0=DCT[:], scalar1=0.0, scalar2=None,
                                                                 op0=ALU.min), reads=[RDCT], writes=[RDCT])
                        S.op("vector", lambda e: e.tensor_scalar(out=GT[:], in0=GT[:], scalar1=0.0, scalar2=None,
                                                                 op0=ALU.min), reads=[RGT], writes=[RGT])
                        S.op("scalar", lambda e: e.activation(out=DCT[:], in_=DCT[:], func=AF.Exp), reads=[RDCT], writes=[RDCT])
                        S.op("scalar", lambda e: e.activation(out=GT[:], in_=GT[:], func=AF.Exp), reads=[RGT], writes=[RGT])
                        S.op("vector", lambda e: e.tensor_tensor(
                            out=DCT[:], in0=DCT[:], in1=mincl.unsqueeze(1).to_broadcast([64, 2 * NCH, CH]), op=ALU.mult),
                            reads=[RDCT, Rc], writes=[RDCT])
                        S.op("vector", lambda e: e.tensor_tensor(
                            out=GT[:], in0=GT[:], in1=mstrict_ts.unsqueeze(1).to_broadcast([64, 2 * NCH, CH]), op=ALU.mult),
                            reads=[RGT, Rc], writes=[RGT])
                        S.op("vector", lambda e: e.tensor_tensor(
                            out=GT[:], in0=GT[:],
                            in1=nbeta[0][:].rearrange("p h c -> p (h c)").unsqueeze(2).to_broadcast([64, 2 * NCH, CH]),
                            op=ALU.mult), reads=[RGT, nbeta[1]], writes=[RGT])

                        stage(4)
                        for hh in range(2):
                            hl = p * 2 + hh
                            rdW = [RW, X[xb][1]]

                            def nextpin():
                                i = pin_i[0] % 2
                                pin_i[0] += 1
                                return Pin[i]
                            dsts = [QS, KS, CV]
                            for s_ in range(3):
                                Pq, RPq = nextpin()
                                S.op("tensor", mm_group(Pq[:], lambda kc, c0=(hh * 4 + s_) * 128: W[:, kc, c0:c0 + 128], xb),
                                     reads=rdW, writes=[RPq])
                                R_, RR_ = RAW[hh][s_]
                                S.op("gpsimd", lambda e, R_=R_: e.tensor_copy(out=R_[:, 0:3], in_=R_[:, TT:TT + 3]),
                                     reads=[RR_], writes=[RR_])
                                S.op("scalar", lambda e, R_=R_, Pq=Pq: e.copy(out=R_[:, 3:3 + TT], in_=Pq[:]),
                                     reads=[RPq, RR_], writes=[RR_])
                                Dd, RDd = dsts[s_]
                                S.op("vector", lambda e, R_=R_, Dd=Dd, s_=s_, hl=hl: e.tensor_scalar(
                                    out=Dd[:], in0=R_[:, 3:3 + TT], scalar1=cw[:, s_, hl, 3:4], scalar2=None, op0=ALU.mult),
                                    reads=[RR_, Rcw], writes=[RDd])
                                for j in (2, 1, 0):
                                    S.op("vector", lambda e, R_=R_, Dd=Dd, s_=s_, hl=hl, j=j: e.scalar_tensor_tensor(
                                        out=Dd[:], in0=R_[:, j:j + TT], scalar=cw[:, s_, hl, j:j + 1], in1=Dd[:],
                                        op0=ALU.mult, op1=ALU.add), reads=[RR_, Rcw, RDd], writes=[RDd])
                                if s_ < 2:
                                    S.op("scalar", lambda e, Dd=Dd: e.activation(out=Dd[:], in_=Dd[:], func=AF.Silu),
                                         reads=[RDd], writes=[RDd])
                                else:
                                    S.op("scalar", lambda e, Dd=Dd: e.activation(out=VSb[0][:], in_=Dd[:], func=AF.Silu),
                                         reads=[RDd], writes=[VSb[1]])
                            Pz, RPz = nextpin()
                            S.op("tensor", mm_group(Pz[:], lambda kc, c0=(hh * 4 + 3) * 128: W[:, kc, c0:c0 + 128], xb),
                                 reads=rdW, writes=[RPz])
                            S.op("scalar", lambda e, Pz=Pz: e.activation(out=ZS[0][:], in_=Pz[:], func=AF.Silu),
                                 reads=[RPz], writes=[ZS[1]])
                            stage(5)
                            for which, (src, Rsrc) in enumerate([QS, KS]):
                                S.op("scalar", lambda e, src=src: e.activation(out=SQ[0][:], in_=src[:], func=AF.Square),
                                     reads=[Rsrc], writes=[SQ[1]])
                                S.op("tensor", lambda e: e.matmul(PM[:], cst[:, C_ONES:C_ONES + 128], SQ[0][:],
                                                                  start=True, stop=True), reads=[SQ[1], Rc], writes=[RPM])
                                S.op("scalar", lambda e: e.activation(out=RS[0][:], in_=PM[:], func=AF.Ln,
                                                                      bias=cx.eps_rms[:, 2:3]),
                                     reads=[RPM, cx.Reps], writes=[RS[1]])
                                if which == 0:
                                    S.op("scalar", lambda e: e.activation(out=RS[0][:], in_=RS[0][:], func=AF.Exp, scale=-0.5,
                                                                          bias=lnsc[:, 0:1]),
                                         reads=[RS[1], Rlnsc], writes=[RS[1]])
                                    S.op("vector", lambda e: e.tensor_tensor(out=QN[0][:], in0=QS[0][:], in1=RS[0][:],
                                                                             op=ALU.mult),
                                         reads=[QS[1], RS[1]], writes=[QN[1]])
                                    S.op("gpsimd", lambda e: e.tensor_copy(out=qTb[0][:], in_=QN[0][:]),
                                         reads=[QN[1]], writes=[qTb[1]])
                                    S.op("gpsimd", lambda e, hh=hh: e.tensor_tensor(
                                        out=qdb[0][:], in0=QN[0][:],
                                        in1=ECB[:, hh * NCH:(hh + 1) * NCH, :].rearrange("p c t -> p (c t)"), op=ALU.mult),
                                        reads=[QN[1], RECB], writes=[qdb[1]])
                                else:
                                    S.op("scalar", lambda e: e.activation(out=RS[0][:], in_=RS[0][:], func=AF.Exp, scale=-0.5),
                                         reads=[RS[1]], writes=[RS[1]])
                                    S.op("vector", lambda e: e.tensor_tensor(out=kTb[0][:], in0=KS[0][:], in1=RS[0][:],
                                                                             op=ALU.mult),
                                         reads=[KS[1], RS[1]], writes=[kTb[1]])
                            stage(6)
                            def tr_fn(src):
                                def fn(e):
                                    ins = None
                                    for c in range(NCH):
                                        ins = e.transpose(Ptr[:, c, :], src[:, c * CH:(c + 1) * CH], idb[:])
                                    return ins
                                return fn

                            def bc(t_, hh=hh):
                                return t_[0][:, hh, :].unsqueeze(2).to_broadcast([64, NCH, 128])
                            S.op("tensor", tr_fn(kTb[0]), reads=[kTb[1], Ridb], writes=[RPtr])
                            S.op("vector", lambda e, b_=bc(ekd): e.tensor_tensor(out=KD[0][:], in0=Ptr[:], in1=b_, op=ALU.mult),
                                 reads=[RPtr, ekd[1]], writes=[KD[1]])
                            S.op("vector", lambda e, b_=bc(nbe): e.tensor_tensor(out=KBE[0][:], in0=Ptr[:], in1=b_, op=ALU.mult),
                                 reads=[RPtr, nbe[1]], writes=[KBE[1]])
                            S.op("tensor", tr_fn(VSb[0]), reads=[VSb[1], Ridb], writes=[RPtr])
                            S.op("vector", lambda e, b_=bc(beta): e.tensor_tensor(out=BV[0][:], in0=Ptr[:], in1=b_, op=ALU.mult),
                                 reads=[RPtr, beta[1]], writes=[BV[1]])
                            stage(7)
                            def kkf(e):
                                ins = None
                                for c in range(NCH):
                                    ins = e.matmul(P5[0:64, c * CH:(c + 1) * CH], kTb[0][:, c * CH:(c + 1) * CH],
                                                   kTb[0][:, c * CH:(c + 1) * CH], start=True, stop=True)
                                return ins
                            S.op("tensor", kkf, reads=[kTb[1]], writes=[RP5])
                            S.op("vector", lambda e, hh=hh: e.tensor_tensor(
                                out=Nm[0][0][:], in0=v3(P5[0:64, :]), in1=GT[:, hh * NCH:(hh + 1) * NCH, :], op=ALU.mult),
                                reads=[RP5, RGT], writes=[Nm[0][1]])

                            def qkf(e):
                                ins = None
                                for c in range(NCH):
                                    ins = e.matmul(P6[0:64, c * CH:(c + 1) * CH], kTb[0][:, c * CH:(c + 1) * CH],
                                                   qTb[0][:, c * CH:(c + 1) * CH], start=True, stop=True)
                                return ins
                            S.op("tensor", qkf, reads=[kTb[1], qTb[1]], writes=[RP6])
                            S.op("vector", lambda e, hh=hh: e.tensor_tensor(
                                out=QKT[0][:], in0=v3(P6[0:64, :]), in1=DCT[:, hh * NCH:(hh + 1) * NCH, :], op=ALU.mult),
                                reads=[RP6, RDCT], writes=[QKT[1]])
                            def trn(e):
                                ins = None
                                for c in range(NCH):
                                    ins = e.transpose(Ptr[:, c, 0:CH], Nm[0][0][:, c, :], idb[0:64, 0:64])
                                return ins
                            S.op("tensor", trn, reads=[Nm[0][1], Ridb], writes=[RPtr])
                            S.op("scalar", lambda e: e.copy(out=Bm[0][0][:], in_=Ptr[:, :, 0:CH]), reads=[RPtr], writes=[Bm[0][1]])
                            S.op("vector", lambda e: e.tensor_tensor(
                                out=Xm[0][0][:], in0=Ptr[:, :, 0:CH],
                                in1=cst[0:64, C_ID:C_ID + 64].unsqueeze(1).to_broadcast([64, NCH, CH]), op=ALU.add),
                                reads=[RPtr, Rc], writes=[Xm[0][1]])
                            stage(8)
                            cur = 0
                            for lvl in range(1, 6):
                                nxt = 1 - cur
                                Nc, Bc, Xc = Nm[cur], Bm[cur], Xm[cur]
                                Nn, Bn, Xn = Nm[nxt], Bm[nxt], Xm[nxt]

                                def sqN(e, Nc=Nc, Bc=Bc):
                                    ins = None
                                    for c in range(NCH):
                                        ins = e.matmul(P5[0:64, c * CH:(c + 1) * CH], Bc[0][:, c, :], Nc[0][:, c, :],
                                                       start=True, stop=True)
                                    return ins
                                S.op("tensor", sqN, reads=[Nc[1], Bc[1]], writes=[RP5])
                                S.op("scalar", lambda e, Nn=Nn: e.copy(out=Nn[0][:], in_=v3(P5[0:64, :])),
                                     reads=[RP5], writes=[Nn[1]])
                                if lvl < 5:
                                    def sqB(e, Nc=Nc, Bc=Bc):
                                        ins = None
                                        for c in range(NCH):
                                            ins = e.matmul(P6[0:64, c * CH:(c + 1) * CH], Nc[0][:, c, :], Bc[0][:, c, :],
                                                           start=True, stop=True)
                                        return ins
                                    S.op("tensor", sqB, reads=[Nc[1], Bc[1]], writes=[RP6])
                                    S.op("scalar", lambda e, Bn=Bn: e.copy(out=Bn[0][:], in_=v3(P6[0:64, :])),
                                         reads=[RP6], writes=[Bn[1]])

                                def xup(e, Nn=Nn, Xc=Xc):
                                    ins = None
                                    for c in range(NCH):
                                        ins = e.matmul(Pcb[0:64, c * CH:(c + 1) * CH], Nn[0][:, c, :], Xc[0][:, c, :],
                                                       start=True, stop=True)
                                    return ins
                                S.op("tensor", xup, reads=[Nn[1], Xc[1]], writes=[RPcb])
                                S.op("vector", lambda e, Xn=Xn, Xc=Xc: e.tensor_tensor(
                                    out=Xn[0][:], in0=v3(Pcb[0:64, :]), in1=Xc[0][:], op=ALU.add),
                                    reads=[RPcb, Xc[1]], writes=[Xn[1]])
                                cur = nxt
                            XT = Xm[cur]
                            stage(9)
                            def wtf(e, XT=XT):
                                ins = None
                                for c in range(NCH):
                                    ins = e.matmul(P5[:, c * CH:(c + 1) * CH], KBE[0][:, c, :], XT[0][:, c, :],
                                                   start=True, stop=True)
                                return ins
                            S.op("tensor", wtf, reads=[KBE[1], XT[1]], writes=[RP5])
                            S.op("scalar", lambda e: e.copy(out=WT[0][:], in_=P5[:]), reads=[RP5], writes=[WT[1]])
                            Pu = Pcb[:].rearrange("p (a m) -> p a m", m=128)
                            for c in range(NCH):
                                ui = c % 2
                                RPu = RPcb

                                def uf(e, c=c, ui=ui, XT=XT, hh=hh):
                                    e.matmul(Pu[0:64, ui, :], XT[0][:, c, :], BV[0][:, c, :], start=True, stop=False)
                                    return e.matmul(Pu[0:64, ui, :], WT[0][:, c * CH:(c + 1) * CH], Sb[hh][0][:],
                                                    start=False, stop=True)
                                S.op("tensor", uf, reads=[XT[1], BV[1], WT[1], Sb[hh][1]], writes=[RPu])
                                S.op("scalar", lambda e, ui=ui: e.copy(out=U[ui][0][:], in_=Pu[0:64, ui, :]),
                                     reads=[RPu], writes=[U[ui][1]])

                                def of(e, c=c, ui=ui, hh=hh):
                                    e.matmul(Po[:, c * CH:(c + 1) * CH], Sb[hh][0][:], qdb[0][:, c * CH:(c + 1) * CH],
                                             start=True, stop=False)
                                    return e.matmul(Po[:, c * CH:(c + 1) * CH], U[ui][0][:], QKT[0][:, c, :],
                                                    start=False, stop=True)
                                S.op("tensor", of, reads=[Sb[hh][1], qdb[1], U[ui][1], QKT[1]], writes=[RPo])
                                S.op("tensor", lambda e, c=c, ui=ui: e.matmul(Pu[:, 2 + ui, :], KD[0][:, c, :], U[ui][0][:],
                                                                              start=True, stop=True),
                                     reads=[KD[1], U[ui][1]], writes=[RPu])
                                S.op("vector", lambda e, c=c, ui=ui, hh=hh: e.scalar_tensor_tensor(
                                    out=Sf[hh][0][:], in0=Sf[hh][0][:], scalar=ELAST[:, hh, c:c + 1], in1=Pu[:, 2 + ui, :],
                                    op0=ALU.mult, op1=ALU.add), reads=[Sf[hh][1], RELAST, RPu], writes=[Sf[hh][1]])
                                S.op("scalar", lambda e, hh=hh: e.copy(out=Sb[hh][0][:], in_=Sf[hh][0][:]),
                                     reads=[Sf[hh][1]], writes=[Sb[hh][1]])
                            stage(10)
                            S.op("scalar", lambda e: e.copy(out=Ot[0][:], in_=Po[:]), reads=[RPo], writes=[Ot[1]])
                            S.op("scalar", lambda e: e.activation(out=SQ[0][:], in_=Po[:], func=AF.Square),
                                 reads=[RPo], writes=[SQ[1]])
                            S.op("tensor", lambda e: e.matmul(PM[:], cst[:, C_ONES:C_ONES + 128], SQ[0][:], start=True, stop=True),
                                 reads=[SQ[1], Rc], writes=[RPM])
                            S.op("scalar", lambda e: e.activation(out=RS[0][:], in_=PM[:], func=AF.Ln, scale=1.0 / 128.0,
                                                                  bias=cx.eps_rms[:, 0:1]),
                                 reads=[RPM, cx.Reps], writes=[RS[1]])
                            S.op("scalar", lambda e: e.activation(out=RS[0][:], in_=RS[0][:], func=AF.Exp, scale=-0.5),
                                 reads=[RS[1]], writes=[RS[1]])
                            S.op("vector", lambda e: e.tensor_tensor(out=Ot[0][:], in0=Ot[0][:], in1=RS[0][:], op=ALU.mult),
                                 reads=[Ot[1], RS[1]], writes=[Ot[1]])
                            yb = y_i[0] % 2
                            y_i[0] += 1
                            S.op("vector", lambda e, yb=yb: e.scalar_tensor_tensor(
                                out=Yb[yb][0][:], in0=Ot[0][:], scalar=nw[:, 0:1], in1=ZS[0][:], op0=ALU.mult, op1=ALU.mult),
                                reads=[Ot[1], ZS[1], Rnw], writes=[Yb[yb][1]])
                            S.dma("sync", f"st_y{yb}", lambda e, yb=yb, shard=shard, hl=hl, off=off: e.dma_start(
                                out=yT[shard, hl * 128:(hl + 1) * 128, off:off + TT], in_=Yb[yb][0][:]),
                                reads=[Yb[yb][1]], writes=[Res("y")])
        try:
            _body()
        except StopBuild:
            pass
        for en in ENGS:
            S.drain(en)
        S.emit()
        cx.stack = old_stack


def build_program(phases):
    nc = bass.Bass("TRN2", target_bir_lowering=False)
    io = {}

    def din(name, shape, dt):
        io[name] = nc.dram_tensor(name, list(shape), dt, kind="ExternalInput").ap()
        return io[name]

    def dout(name, shape, dt):
        io[name] = nc.dram_tensor(name, list(shape), dt, kind="ExternalOutput").ap()
        return io[name]

    with contextlib.ExitStack() as st:
        cx = Ctx(nc, st)
        cst_ap = din("cst", [128, C_W], F32)
        load_consts(cx, cst_ap)
        eps, Reps = cx.sb("eps_rms", [128, 4], F32)
        cx.S.op("vector", lambda e: e.memset(eps[:, 0:1], RMS_EPS), writes=[Reps])
        cx.S.op("vector", lambda e: e.memset(eps[:, 1:2], LN_EPS), writes=[Reps])
        cx.S.op("vector", lambda e: e.memset(eps[:, 2:3], L2_EPS), writes=[Reps])
        cx.S.op("vector", lambda e: e.memset(eps[:, 3:4], 1.0), writes=[Reps])
        cx.eps_rms, cx.Reps = eps, Reps
        if "a0" in phases:
            xT = din("xT", [NCORE, D, TOKC], F32)
            win = din("a_win", [D, HPC, 4, 128], F32)
            lbraw = din("a_lbraw", [3, HPC, 128], F32)
            normw = din("a_normw", [128], F32)
            y0T = dout("y0T", [NCORE, HPC * 128, TOKC], BF16)
            phase_a0(cx, xT, True, win, lbraw, normw, y0T, Res("y0T"))
        if "a1" in phases:
            x1T = din("x1T", [NCORE, D, TOKC], BF16)
            bwin = din("b_win", [D, HPC, 4, 128], F32)
            bwba = din("b_wba", [128, KC, 8], F32)
            bconv = din("b_convw", [128, 3, HPC, 4], F32)
            bhv = din("b_hv", [64, 2, HPC], F32)
            bnormw = din("b_normw", [128], F32)
            y1T = dout("y1T", [NCORE, HPC * 128, TOKC], BF16)
            phase_a1(cx, x1T, bwin, bwba, bconv, bhv, bnormw, y1T)
        if "b0" in phases or "b1" in phases:
            last = "b1" in phases
            yT_own = din("yT_own", [D, TOKC], BF16)
            wout = din("wout", [D, D], F32)
            xres = din("xres", [D, TOKC], F32)
            lng = din("lng", [D], F32)
            lnb = din("lnb", [D], F32)
            zs = nc.dram_tensor("zs", [D, TOKC], F32, kind="Internal").ap()
            o32 = dout("o32", [D, TOKC], F32)
            obf = None if last else dout("obf", [D, TOKC], BF16)
            phase_b(cx, yT_own, wout, xres, lng, lnb, zs, o32, obf)
        cx.S.drain("sync")
        cx.S.emit()
    return nc


_PROG_CACHE = {}


def _prog(key):
    if key not in _PROG_CACHE:
        _PROG_CACHE[key] = build_program([key])
    return _PROG_CACHE[key]


def _launch(key, in_maps):
    res = run_bass_kernel_spmd(_prog(key), in_maps, core_ids=list(range(NCORE)))
    return res.results


def _core_inputs(c, a_w_in, a_lower_bounds, a_norm_w, b_w_in, b_conv_w, b_a_log, b_dt_bias, b_norm_w):
    awin = a_w_in[0].reshape(D, 4, 32, 128)
    lbr = a_lower_bounds.reshape(3, 32, 128)
    bw = b_w_in[0]
    bwin = bw[:, :4 * D].reshape(D, 4, 32, 128)
    conv = b_conv_w[0].reshape(4, 3, 32, 128)
    h0 = HPC * c
    hv = np.stack([b_a_log[0][h0:h0 + HPC], b_dt_bias[0][h0:h0 + HPC]])[None].repeat(64, 0)
    wba = np.concatenate([bw[:, 4 * D + h0:4 * D + h0 + HPC], bw[:, 4 * D + 32 + h0:4 * D + 32 + h0 + HPC]], axis=1)
    return {
        "a_win": np.ascontiguousarray(awin[:, :, h0:h0 + HPC, :].transpose(0, 2, 1, 3)),
        "a_lbraw": np.ascontiguousarray(lbr[:, h0:h0 + HPC, :]),
        "a_normw": np.ascontiguousarray(a_norm_w[0]),
        "b_win": np.ascontiguousarray(bwin[:, :, h0:h0 + HPC, :].transpose(0, 2, 1, 3)),
        "b_wba": np.ascontiguousarray(wba.reshape(KC, 128, 8).transpose(1, 0, 2)),
        "b_convw": np.ascontiguousarray(conv[:, :, h0:h0 + HPC, :].transpose(3, 1, 2, 0)),
        "b_hv": np.ascontiguousarray(hv.astype(np.float32)),
        "b_normw": np.ascontiguousarray(b_norm_w[0]),
    }


def kernel(x, a_w_in, a_lower_bounds, a_norm_w, a_w_out, b_w_in, b_conv_w, b_a_log, b_dt_bias, b_norm_w,
           b_w_out, ln_g, ln_b):
    f = lambda a: np.asarray(a, dtype=np.float32)
    x, a_w_in, a_lower_bounds, a_norm_w, a_w_out = f(x), f(a_w_in), f(a_lower_bounds), f(a_norm_w), f(a_w_out)
    b_w_in, b_conv_w, b_a_log, b_dt_bias, b_norm_w, b_w_out = (f(b_w_in), f(b_conv_w), f(b_a_log), f(b_dt_bias),
                                                               f(b_norm_w), f(b_w_out))
    ln_g, ln_b = f(ln_g), f(ln_b)
    cst = host_consts()
    xT = np.ascontiguousarray(x.reshape(NCORE, TOKC, D).transpose(0, 2, 1))
    ci = [_core_inputs(c, a_w_in, a_lower_bounds, a_norm_w, b_w_in, b_conv_w, b_a_log, b_dt_bias, b_norm_w)
          for c in range(NCORE)]
    awo = np.ascontiguousarray(a_w_out[0])
    bwo = np.ascontiguousarray(b_w_out[0])
    r = _launch("a0", [{"cst": cst, "xT": xT, "a_win": ci[c]["a_win"], "a_lbraw": ci[c]["a_lbraw"],
                        "a_normw": ci[c]["a_normw"]} for c in range(NCORE)])
    y0 = [r[c]["y0T"] for c in range(NCORE)]
    r = _launch("b0", [{"cst": cst, "yT_own": np.ascontiguousarray(np.concatenate([y0[s][c] for s in range(NCORE)], axis=0)),
                        "wout": awo, "xres": xT[c], "lng": np.ascontiguousarray(ln_g[0]),
                        "lnb": np.ascontiguousarray(ln_b[0])} for c in range(NCORE)])
    x1res = [r[c]["o32"] for c in range(NCORE)]
    x1T = np.ascontiguousarray(np.stack([r[c]["obf"] for c in range(NCORE)]))
    r = _launch("a1", [{"cst": cst, "x1T": x1T, "b_win": ci[c]["b_win"], "b_wba": ci[c]["b_wba"],
                        "b_convw": ci[c]["b_convw"], "b_hv": ci[c]["b_hv"], "b_normw": ci[c]["b_normw"]}
                       for c in range(NCORE)])
    y1 = [r[c]["y1T"] for c in range(NCORE)]
    r = _launch("b1", [{"cst": cst, "yT_own": np.ascontiguousarray(np.concatenate([y1[s][c] for s in range(NCORE)], axis=0)),
                        "wout": bwo, "xres": np.ascontiguousarray(x1res[c]), "lng": np.ascontiguousarray(ln_g[1]),
                        "lnb": np.ascontiguousarray(ln_b[1])} for c in range(NCORE)])
    out = np.concatenate([np.ascontiguousarray(r[c]["o32"].T) for c in range(NCORE)], axis=0)
    return out.reshape(2, SEQ, D).astype(np.float32)
```
